# Optimizing a Trainium2 kernel written in Bass

```python
import math
import jax, jax.numpy as jnp
from jax import lax
import numpy as np

D_MODEL = 1024
BATCH = 1
SEQ = 16384
DEPTH = 4

N_MIXERS = 3
N_MEM = 256
GRID_W = 64
HEAD_DIM = 64
ROPE_THETA = 10000.0
LN_EPS = 1e-5
RMS_EPS = 1e-6
MASK_VALUE = -1e30

FNET_GROUPS = 4
GQA_Q_HEADS = D_MODEL // HEAD_DIM
GQA_KV_HEADS = GQA_Q_HEADS // 4
Q_BLOCK = 128
DIL_PAIRS = ((128, 1), (512, 4), (2048, 16))
DIL_GROUPS = len(DIL_PAIRS)
DIL_HEADS = D_MODEL // HEAD_DIM
DIL_WIDTH = DIL_GROUPS * DIL_HEADS * HEAD_DIM
MEM_HEADS = 4
MEM_WIDTH = MEM_HEADS * HEAD_DIM
BRANCH_WIDTH = D_MODEL
INNER = BRANCH_WIDTH + MEM_WIDTH
TAIL = INNER + MEM_WIDTH
COLS_A = TAIL
COLS_B = GQA_Q_HEADS * HEAD_DIM + 2 * GQA_KV_HEADS * HEAD_DIM + TAIL
COLS_C = 3 * DIL_WIDTH + TAIL
N_LAYERS_A = (DEPTH + 2) // 3
N_LAYERS_B = (DEPTH + 1) // 3
N_LAYERS_C = DEPTH // 3
DEEPNORM_ALPHA = (2.0 * DEPTH) ** 0.25
DEEPNORM_BETA = (8.0 * DEPTH) ** -0.25

kernel_name = "hybrid_fnet_gqa_dilated_encoder"


def layer_norm(x, g, b):
    xf = x.astype(jnp.float32)
    mu = jnp.mean(xf, axis=-1, keepdims=True)
    var = jnp.mean(jnp.square(xf - mu), axis=-1, keepdims=True)
    y = (xf - mu) * lax.rsqrt(var + LN_EPS) * g.astype(jnp.float32) + b.astype(jnp.float32)
    return y.astype(x.dtype)


def rms_norm(x, g):
    xf = x.astype(jnp.float32)
    return xf * lax.rsqrt(jnp.mean(jnp.square(xf), axis=-1, keepdims=True) + RMS_EPS) * g.astype(jnp.float32)


def rope_angles(pos, dim):
    inv_freq = ROPE_THETA ** (-(jnp.arange(0, dim, 2, dtype=jnp.float32) / dim))
    return pos.astype(jnp.float32)[:, None] * inv_freq[None, :]


def apply_rope(x, ang):
    x1, x2 = jnp.split(x, 2, axis=-1)
    c = jnp.cos(ang)[:, None, :]
    s = jnp.sin(ang)[:, None, :]
    return jnp.concatenate([x1 * c - x2 * s, x2 * c + x1 * s], axis=-1)


def apply_axial_rope(x, ang_row, ang_col):
    half = x.shape[-1] // 2
    return jnp.concatenate([apply_rope(x[..., :half], ang_row), apply_rope(x[..., half:], ang_col)], axis=-1)


def fourier_mix(h):
    B, S, D = h.shape
    hg = h.astype(jnp.float32).reshape(B, S, FNET_GROUPS, D // FNET_GROUPS)
    y = jnp.fft.fftn(hg, axes=(1, 3), norm="ortho").real
    return y.astype(jnp.float32).reshape(B, S, D)


def gqa_block_attention(q, k, v):
    B, S, Hq, Dh = q.shape
    Hkv = k.shape[2]
    G = Hq // Hkv
    nb = S // Q_BLOCK
    qb = q.reshape(B, nb, Q_BLOCK, Hkv, G, Dh).transpose(1, 0, 3, 4, 2, 5)
    kt = k.transpose(0, 2, 1, 3)
    vt = v.transpose(0, 2, 1, 3)
    scale = Dh ** -0.5

    def one_block(qblk):
        s = jnp.einsum("bhgqd,bhkd->bhgqk", qblk, kt) * scale
        p = jax.nn.softmax(s, axis=-1)
        return jnp.einsum("bhgqk,bhkd->bhgqd", p, vt)

    o = lax.map(one_block, qb)
    return o.transpose(1, 0, 4, 2, 3, 5).reshape(B, S, Hq * Dh)


def dilated_group_attention(q, k, v, dil, side):
    B, S, H, Dh = q.shape
    L = S // dil
    nb = -(-L // side)
    Lp = nb * side

    def to_classes(t):
        t = t.reshape(B, L, dil, H, Dh).transpose(0, 2, 1, 3, 4)
        return jnp.pad(t, ((0, 0), (0, 0), (0, Lp - L), (0, 0), (0, 0)))

    qc, kc, vc = to_classes(q), to_classes(k), to_classes(v)
    qb = qc.reshape(B, dil, nb, side, H, Dh)

    def band(t):
        tp = jnp.pad(t, ((0, 0), (0, 0), (side, side), (0, 0), (0, 0)))
        return jnp.concatenate(
            [tp[:, :, i * side: i * side + Lp].reshape(B, dil, nb, side, H, Dh) for i in range(3)], axis=3)

    kb, vb = band(kc), band(vc)
    qi = jnp.arange(side)
    kj = jnp.arange(3 * side) - side
    in_window = jnp.abs(kj[None, :] - qi[:, None]) <= side
    kabs = jnp.arange(nb)[:, None] * side + kj[None, :]
    valid = (kabs >= 0) & (kabs < L)
    mask = in_window[None, :, :] & valid[:, None, :]

    s = jnp.einsum("bcnqhd,bcnkhd->bcnhqk", qb, kb) * (Dh ** -0.5)
    s = jnp.where(mask[None, None, :, None, :, :], s, MASK_VALUE)
    lse = jax.nn.logsumexp(s, axis=-1)
    p = jnp.exp(s - lse[..., None])
    o = jnp.einsum("bcnhqk,bcnkhd->bcnqhd", p, vb)
    o = o.reshape(B, dil, Lp, H, Dh)[:, :, :L].transpose(0, 2, 1, 3, 4).reshape(B, S, H, Dh)
    lse = lse.transpose(0, 1, 2, 4, 3).reshape(B, dil, Lp, H)[:, :, :L]
    lse = lse.transpose(0, 2, 1, 3).reshape(B, S, H)
    return o, lse


def memory_attention(qm, mk, mv):
    s = jnp.einsum("bshd,bmhd->bhsm", qm, mk) * (qm.shape[-1] ** -0.5)
    p = jax.nn.softmax(s, axis=-1)
    o = jnp.einsum("bhsm,bmhd->bshd", p, mv)
    return o.reshape(qm.shape[0], qm.shape[1], -1)


def setup_inputs(seed: int = 0) -> dict:
    key = jax.random.key(seed)
    ks = jax.random.split(key, 16)
    f32 = jnp.float32
    w_in_scale = D_MODEL ** -0.5
    return {
        "x": jax.random.normal(ks[0], (BATCH, SEQ, D_MODEL), f32),
        "mem": jax.random.normal(ks[1], (BATCH, N_MEM, D_MODEL), f32),
        "ln_in_g": 1.0 + 0.02 * jax.random.normal(ks[2], (D_MODEL,), f32),
        "ln_in_b": 0.02 * jax.random.normal(ks[3], (D_MODEL,), f32),
        "w_mem_kv": jax.random.normal(ks[4], (D_MODEL, 2 * MEM_WIDTH), f32) * w_in_scale,
        "w_in_a": jax.random.normal(ks[5], (N_LAYERS_A, D_MODEL, COLS_A), f32) * w_in_scale,
        "w_in_b": jax.random.normal(ks[6], (N_LAYERS_B, D_MODEL, COLS_B), f32) * w_in_scale,
        "q_norm_g": 1.0 + 0.02 * jax.random.normal(ks[7], (N_LAYERS_B, HEAD_DIM), f32),
        "k_norm_g": 1.0 + 0.02 * jax.random.normal(ks[8], (N_LAYERS_B, HEAD_DIM), f32),
        "w_in_c": jax.random.normal(ks[9], (N_LAYERS_C, D_MODEL, COLS_C), f32) * w_in_scale,
        "w_out": jax.random.normal(ks[10], (DEPTH, INNER, D_MODEL), f32) * (INNER ** -0.5) * DEEPNORM_BETA,
        "ln_g": 1.0 + 0.02 * jax.random.normal(ks[11], (DEPTH, D_MODEL), f32),
        "ln_b": 0.02 * jax.random.normal(ks[12], (DEPTH, D_MODEL), f32),
    }


def reference(x, mem, ln_in_g, ln_in_b, w_mem_kv, w_in_a, w_in_b, q_norm_g, k_norm_g, w_in_c, w_out, ln_g, ln_b):
    B, S, D = x.shape
    ROWS = S // GRID_W
    t = jnp.arange(S)
    row = jnp.repeat(jnp.arange(ROWS), GRID_W, total_repeat_length=S)
    col = jnp.tile(jnp.arange(GRID_W), ROWS)
    ang_1d = rope_angles(t, HEAD_DIM)
    ang_row = rope_angles(row, HEAD_DIM // 2)
    ang_col = rope_angles(col, HEAD_DIM // 2)

    mkv = jnp.matmul(mem, w_mem_kv).astype(jnp.float32).reshape(B, N_MEM, 2, MEM_HEADS, HEAD_DIM)
    mk, mv = mkv[:, :, 0], mkv[:, :, 1]

    h = layer_norm(x, ln_in_g, ln_in_b)
    qw = GQA_Q_HEADS * HEAD_DIM
    kvw = GQA_KV_HEADS * HEAD_DIM
    for i in range(DEPTH):
        kind = i % N_MIXERS
        j = i // N_MIXERS
        if kind == 0:
            proj = jnp.matmul(h, w_in_a[j]).astype(jnp.float32)
            branch = fourier_mix(h)
        elif kind == 1:
            proj = jnp.matmul(h, w_in_b[j]).astype(jnp.float32)
            q = proj[..., :qw].reshape(B, S, GQA_Q_HEADS, HEAD_DIM)
            k = proj[..., qw:qw + kvw].reshape(B, S, GQA_KV_HEADS, HEAD_DIM)
            v = proj[..., qw + kvw:qw + 2 * kvw].reshape(B, S, GQA_KV_HEADS, HEAD_DIM)
            q = apply_axial_rope(rms_norm(q, q_norm_g[j]), ang_row, ang_col)
            k = apply_axial_rope(rms_norm(k, k_norm_g[j]), ang_row, ang_col)
            branch = gqa_block_attention(q, k, v)
        else:
            proj = jnp.matmul(h, w_in_c[j]).astype(jnp.float32)
            qkv = proj[..., :3 * DIL_WIDTH].reshape(B, S, 3, DIL_GROUPS, DIL_HEADS, HEAD_DIM)
            outs = []
            lses = []
            for g, (window, dil) in enumerate(DIL_PAIRS):
                qg = apply_rope(qkv[:, :, 0, g], ang_1d)
                kg = apply_rope(qkv[:, :, 1, g], ang_1d)
                og, lg = dilated_group_attention(qg, kg, qkv[:, :, 2, g], dil, window // (2 * dil))
                outs.append(og)
                lses.append(lg)
            wts = jax.nn.softmax(jnp.stack(lses, axis=0), axis=0)
            branch = jnp.sum(wts[..., None] * jnp.stack(outs, axis=0), axis=0).reshape(B, S, BRANCH_WIDTH)
        tail = proj[..., proj.shape[-1] - TAIL:]
        gate = tail[..., :INNER]
        qm = tail[..., INNER:].reshape(B, S, MEM_HEADS, HEAD_DIM)
        mem_out = memory_attention(qm, mk, mv)
        y = jnp.concatenate([branch, mem_out], axis=-1) * jax.nn.silu(gate)
        y = jnp.matmul(y.astype(h.dtype), w_out[i])
        h = layer_norm(DEEPNORM_ALPHA * h + y.astype(h.dtype), ln_g[i], ln_b[i])
    return h
```

```python
import contextlib
import math
import numpy as np
import ml_dtypes
import concourse.bass as bass
import concourse.mybir as mybir
from concourse.bass_utils import run_bass_kernel_spmd

F32 = mybir.dt.float32
BF16 = mybir.dt.bfloat16
AF = mybir.ActivationFunctionType
ALU = mybir.AluOpType
AX = mybir.AxisListType

NCORES = 8
SEQ = 16384
D = 1024
NT = SEQ // NCORES
ALPHA = (2.0 * 4) ** 0.25
LN_EPS = 1e-5
RMS_EPS = 1e-6

ENGS = ["pe", "act", "dve", "pool", "sp"]


class T:
    def __init__(self, name, ap=None):
        self.name = name
        self.ap = ap
        self.w = []
        self.r = []


class Sched:
    def __init__(self, nc, same_engine_sync=True):
        self.nc = nc
        self.q = {e: [] for e in ENGS}
        self.cnt = {}
        self.seen = {e: {} for e in ENGS}
        self.same = same_engine_sync
        self.stack = contextlib.ExitStack()
        self.sems = {}
        self.dma_keys = []
        self.n_ops = {e: 0 for e in ENGS}
        self.rr = 0

    def sbuf(self, name, shape, dt):
        t = self.stack.enter_context(self.nc.sbuf_tensor(name, list(shape), dt))
        return T(name, t)

    def psum(self, name, shape, dt=F32):
        t = self.stack.enter_context(self.nc.psum_tensor(name, list(shape), dt))
        return T(name, t)

    def dram(self, name, shape, dt, kind="Internal"):
        t = self.nc.dram_tensor(name, list(shape), dt, kind=kind)
        return T(name, t.ap())

    def view(self, name, ap):
        return T(name, ap)

    def _sem(self, key):
        if key not in self.sems:
            self.sems[key] = self.stack.enter_context(self.nc.semaphore("s_" + key))
            self.cnt[key] = 0
        return self.sems[key]

    def _deps(self, eng, reads, writes):
        need = {}
        for t in reads:
            for (k, v) in t.w:
                need[k] = max(need.get(k, 0), v)
        for t in writes:
            for (k, v) in t.w + t.r:
                need[k] = max(need.get(k, 0), v)
        for k, v in need.items():
            if k == eng and (eng == "pe" or not self.same):
                continue
            if self.seen[eng].get(k, 0) >= v:
                continue
            self.seen[eng][k] = v
            sem = self._sem(k)
            self.q[eng].append(lambda E, sem=sem, v=v: E.wait_ge(sem, v))

    def _post(self, key, val, reads, writes, accum):
        ev = (key, val)
        for t in reads:
            t.r.append(ev)
            if len(t.r) > 64:
                t.r = t.r[-64:]
        for t in writes:
            t.w = [ev]
            t.r = []
        for t in accum:
            t.w = [e for e in t.w if e[0] != key] + [ev]
            t.r = []

    def op(self, eng, meth, *args, reads=(), writes=(), accum=(), **kw):
        reads, writes, accum = list(reads), list(writes), list(accum)
        self._deps(eng, reads, writes + accum)
        sem = self._sem(eng)
        self.cnt[eng] += 1
        v = self.cnt[eng]
        self.q[eng].append(lambda E, meth=meth, args=args, kw=kw, sem=sem: getattr(E, meth)(*args, **kw).then_inc(sem, 1))
        self._post(eng, v, reads, writes, accum)
        self.n_ops[eng] += 1

    def dma(self, eng, key, out_ap, in_ap, reads=(), writes=(), accum=(), **kw):
        reads, writes, accum = list(reads), list(writes), list(accum)
        self._deps(eng, reads, writes + accum)
        key = "d_" + key
        sem = self._sem(key)
        if key not in self.dma_keys:
            self.dma_keys.append(key)
        self.cnt[key] += 16
        v = self.cnt[key]
        self.q[eng].append(
            lambda E, sem=sem, o=out_ap, i=in_ap, kw=kw: E.dma_start(out=o, in_=i, **kw).then_inc(sem, 16))
        self._post(key, v, reads, writes, accum)
        self.n_ops[eng] += 1

    def finish(self, eng="sp"):
        for k in self.dma_keys:
            v = self.cnt[k]
            if self.seen[eng].get(k, 0) >= v:
                continue
            self.seen[eng][k] = v
            sem = self.sems[k]
            self.q[eng].append(lambda E, sem=sem, v=v: E.wait_ge(sem, v))

    def emit(self):
        nc = self.nc
        q = self.q
        with nc.Block() as block:
            @block.tensor
            def _(E):
                for f in q["pe"]:
                    f(E)

            @block.scalar
            def _(E):
                for f in q["act"]:
                    f(E)

            @block.vector
            def _(E):
                for f in q["dve"]:
                    f(E)

            @block.gpsimd
            def _(E):
                for f in q["pool"]:
                    f(E)

            @block.sync
            def _(E):
                for f in q["sp"]:
                    f(E)
        self.stack.close()


class Ctx:
    def __init__(self, S, ident_d):
        self.S = S
        idf = S.sbuf("c_idf", [128, 128], F32)
        S.dma("sp", "c_idf", idf.ap[:], ident_d.ap[:, :], writes=[idf])
        self.ident_f = idf
        idb = S.sbuf("c_idb", [128, 128], BF16)
        S.op("dve", "tensor_copy", out=idb.ap[:], in_=idf.ap[:], reads=[idf], writes=[idb])
        self.ident_b = idb
        ones = S.sbuf("c_ones", [128, 128], F32)
        S.op("dve", "memset", ones.ap[:], 1.0, writes=[ones])
        self.ones_f = ones
        self.wst = [S.sbuf(f"c_wst{i}", [128, 1536], F32) for i in range(2)]
        self.wst_n = 0
        self.cv_n = 0

    def cv_eng(self):
        self.cv_n += 1
        return ["pool", "act"][self.cv_n % 2]


def copy_op(S, eng, out_ap, in_ap, reads, writes=(), accum=()):
    S.op(eng, "copy" if eng == "act" else "tensor_copy", out=out_ap, in_=in_ap, reads=reads, writes=writes, accum=accum)


def load_w_bf16(S, C, name, w_ap, nk, cols, col0=0):
    w = S.sbuf(name, [128, nk, cols], BF16)
    for kc in range(nk):
        for c0 in range(0, cols, 1536):
            cw = min(1536, cols - c0)
            st = C.wst[C.wst_n % 2]
            C.wst_n += 1
            S.dma("sp", "wst%d" % (C.wst_n % 2), st.ap[:, :cw],
                  w_ap[kc * 128:(kc + 1) * 128, col0 + c0:col0 + c0 + cw], writes=[st])
            copy_op(S, C.cv_eng(), w.ap[:, kc, c0:c0 + cw], st.ap[:, :cw], reads=[st], accum=[w])
    return w


def load_bcast(S, name, vec_ap, n):
    t = S.sbuf(name, [128, n], F32)
    S.dma("sp", name, t.ap[:], vec_ap.partition_broadcast(128), writes=[t])
    return t


class LNBufs:
    def __init__(self, S, pfx):
        self.st = [[S.sbuf(f"{pfx}_st{i}_{j}", [128, 1], F32) for j in range(5)] for i in range(2)]
        self.junk = S.sbuf(f"{pfx}_junk", [128, D], F32)
        self.tmp = [S.sbuf(f"{pfx}_tmp{i}", [128, D], F32) for i in range(2)]


def layer_norm_tile(S, B, i, r, out_t, g_rep, b_rep):
    ssum, nmean, ssq, std, rstd = B.st[i % 2]
    tmp = B.tmp[i % 2]
    S.op("dve", "reduce_sum", out=ssum.ap[:], in_=r.ap[:], axis=AX.X, reads=[r], writes=[ssum])
    S.op("dve", "tensor_scalar", out=nmean.ap[:], in0=ssum.ap[:], scalar1=-1.0 / D, scalar2=None,
                                         op0=ALU.mult, reads=[ssum], writes=[nmean])
    S.op("dve", "memset", ssq.ap[:], 0.0, writes=[ssq])
    S.op("act", "activation", out=B.junk.ap[:], in_=r.ap[:], func=AF.Square, bias=nmean.ap[:],
                                       scale=1.0, accum_out=ssq.ap[:],
         reads=[r, nmean], writes=[B.junk], accum=[ssq])
    S.op("act", "activation", out=std.ap[:], in_=ssq.ap[:], func=AF.Sqrt, bias=LN_EPS, scale=1.0 / D,
         reads=[ssq], writes=[std])
    S.op("dve", "reciprocal", out=rstd.ap[:], in_=std.ap[:], reads=[std], writes=[rstd])
    S.op("dve", "tensor_scalar", out=tmp.ap[:], in0=r.ap[:], scalar1=nmean.ap[:], scalar2=rstd.ap[:],
                                         op0=ALU.add, op1=ALU.mult, reads=[r, nmean, rstd], writes=[tmp])
    S.op("pool", "tensor_tensor", out=tmp.ap[:], in0=tmp.ap[:], in1=g_rep.ap[:], op=ALU.mult,
         reads=[tmp, g_rep], writes=[tmp])
    S.op("pool", "tensor_tensor", out=out_t.ap[:], in0=tmp.ap[:], in1=b_rep.ap[:], op=ALU.add,
         reads=[tmp, b_rep], writes=[out_t])


class TrBufs:
    def __init__(self, S, pfx, ptr):
        self.hb = [S.sbuf(f"{pfx}_hb{i}", [128, D], BF16) for i in range(2)]
        self.hTs = [S.sbuf(f"{pfx}_hTs{i}", [128, 8, 512], BF16) for i in range(2)]
        self.ptr = ptr


def transpose_tile(S, C, B, tt, hn, hT_d, tok_off=0):
    hb = B.hb[tt % 2]
    hTs = B.hTs[(tt // 4) % 2]
    sub = tt % 4
    S.op("pool", "tensor_copy", out=hb.ap[:], in_=hn.ap[:], reads=[hn], writes=[hb])
    for kc in range(8):
        S.op("pe", "matmul", B.ptr.ap[:, kc * 128:(kc + 1) * 128], lhsT=hb.ap[:, kc * 128:(kc + 1) * 128],
                                            rhs=C.ident_b.ap[:], start=True, stop=True,
             reads=[hb, C.ident_b], writes=[B.ptr] if kc == 0 else [], accum=[B.ptr] if kc else [])
    S.op("act", "copy", out=hTs.ap[:, :, sub * 128:(sub + 1) * 128],
                                 in_=B.ptr.ap[:].rearrange("p (k t) -> p k t", k=8),
         reads=[B.ptr], accum=[hTs])
    if sub == 3:
        t0 = tok_off + (tt // 4) * 512
        S.dma("sp", "hTout", hT_d.ap.rearrange("(k p) t -> p k t", p=128)[:, :, t0:t0 + 512], hTs.ap[:],
              reads=[hTs], accum=[hT_d])


def phase_ln_in(S, C, x_d, g_ap, b_ap, h_d, hT_d, po):
    g_rep = load_bcast(S, "p0_g", g_ap, D)
    b_rep = load_bcast(S, "p0_b", b_ap, D)
    xs = [S.sbuf(f"p0_x{i}", [128, D], F32) for i in range(2)]
    hn = [S.sbuf(f"p0_hn{i}", [128, D], F32) for i in range(2)]
    LB = LNBufs(S, "p0ln")
    TB = TrBufs(S, "p0tr", po)
    for tt in range(NT // 128):
        x = xs[tt % 2]
        S.dma("sp", f"p0x{tt % 2}", x.ap[:], x_d.ap[tt * 128:(tt + 1) * 128, :], writes=[x])
        layer_norm_tile(S, LB, tt, x, hn[tt % 2], g_rep, b_rep)
        S.dma("sp", f"p0h{tt % 2}", h_d.ap[tt * 128:(tt + 1) * 128, :], hn[tt % 2].ap[:], reads=[hn[tt % 2]],
              accum=[h_d])
        transpose_tile(S, C, TB, tt, hn[tt % 2], hT_d)


def phase_back(S, C, pfx, hT_d, h_d, brT_d, w_tail_ap, w_out_ap, mem_d, wmkv_d, lng_ap, lnb_ap, hn_d, hnT_d, PS, dbg=None):
    pg, pS, pO, pB, po = PS["pg"], PS["pS"], PS["pO"], PS["pB"], PS["po"]
    wt = load_w_bf16(S, C, pfx + "wt", w_tail_ap, 8, 1536)
    wo = load_w_bf16(S, C, pfx + "wo", w_out_ap, 10, 1024)
    wkv = load_w_bf16(S, C, pfx + "wkv", wmkv_d.ap, 8, 512)
    g_rep = load_bcast(S, pfx + "g", lng_ap, D)
    b_rep = load_bcast(S, pfx + "b", lnb_ap, D)
    if dbg is not None and len(dbg) > 2 and dbg[2] == 3:
        return phase_back_rest(S, C, pfx, hT_d, h_d, brT_d, hn_d, hnT_d, PS, dbg, wt, wo, None, None, g_rep, b_rep)
    mem_sb = S.sbuf(pfx + "mem", [128, 2, D], F32)
    S.dma("sp", pfx + "mem", mem_sb.ap[:], mem_d.ap.rearrange("(m p) f -> p m f", p=128), writes=[mem_sb])
    mem_bf = S.sbuf(pfx + "membf", [128, 2, D], BF16)
    S.op("dve", "tensor_copy", out=mem_bf.ap[:], in_=mem_sb.ap[:], reads=[mem_sb], writes=[mem_bf])
    memT = S.sbuf(pfx + "memT", [128, 8, 256], BF16)
    for mt in range(2):
        for kc in range(8):
            S.op("pe", "matmul", po.ap[:, kc * 128:(kc + 1) * 128],
                                                       lhsT=mem_bf.ap[:, mt, kc * 128:(kc + 1) * 128],
                                                       rhs=C.ident_b.ap[:], start=True, stop=True,
                 reads=[mem_bf, C.ident_b], writes=[po] if kc == 0 else [], accum=[po] if kc else [])
        S.op("act", "copy", out=memT.ap[:, :, mt * 128:(mt + 1) * 128],
                                           in_=po.ap[:].rearrange("p (k t) -> p k t", k=8),
             reads=[po], accum=[memT])
    mkT = S.sbuf(pfx + "mkT", [128, 2, 256], BF16)
    for hp in range(2):
        for kc in range(8):
            S.op("pe", "matmul", pg[0].ap[:, 0:256], lhsT=wkv.ap[:, kc, hp * 128:(hp + 1) * 128],
                                                       rhs=memT.ap[:, kc, :], start=(kc == 0), stop=(kc == 7),
                 reads=[wkv, memT], writes=[pg[0]] if kc == 0 else [], accum=[pg[0]] if kc else [])
        S.op("act", "copy", out=mkT.ap[:, hp, :], in_=pg[0].ap[:, 0:256], reads=[pg[0]], accum=[mkT])
    mvx = S.sbuf(pfx + "mvx", [128, 2, 4, 128], BF16)
    S.op("dve", "memset", mvx.ap[:], 0.0, writes=[mvx])
    for hd in range(4):
        c1 = 64 if hd % 2 == 0 else 0
        S.op("dve", "memset", mvx.ap[:, :, hd, c1:c1 + 1], 1.0, accum=[mvx])
    for mc in range(2):
        for kc in range(8):
            S.op("pe", "matmul", pg[1].ap[:, 0:256], lhsT=memT.ap[:, kc, mc * 128:(mc + 1) * 128],
                                                       rhs=wkv.ap[:, kc, 256:512], start=(kc == 0), stop=(kc == 7),
                 reads=[wkv, memT], writes=[pg[1]] if kc == 0 else [], accum=[pg[1]] if kc else [])
        for hd in range(4):
            c0 = 0 if hd % 2 == 0 else 64
            S.op("act", "copy", out=mvx.ap[:, mc, hd, c0:c0 + 64],
                                                             in_=pg[1].ap[:, hd * 64:(hd + 1) * 64],
                 reads=[pg[1]], accum=[mvx])
    return phase_back_rest(S, C, pfx, hT_d, h_d, brT_d, hn_d, hnT_d, PS, dbg, wt, wo, mkT, mvx, g_rep, b_rep)


def phase_back_rest(S, C, pfx, hT_d, h_d, brT_d, hn_d, hnT_d, PS, dbg, wt, wo, mkT, mvx, g_rep, b_rep):
    pg, pS, pO, pB, po = PS["pg"], PS["pS"], PS["pO"], PS["pB"], PS["po"]
    hTc = [S.sbuf(f"{pfx}hTc{i}", [128, 8, 512], BF16) for i in range(2)]
    qmT = S.sbuf(pfx + "qmT", [128, 2, 512], BF16)
    pT = [S.sbuf(f"{pfx}pT{i}", [128, 512], BF16) for i in range(2)]
    Osb = S.sbuf(pfx + "Osb", [128, 512], F32)
    rd = S.sbuf(pfx + "rd", [128, 512], F32)
    moT = S.sbuf(pfx + "moT", [128, 2, 512], F32)
    sg = [S.sbuf(f"{pfx}sg{i}", [128, 512], F32) for i in range(2)]
    brc = [S.sbuf(f"{pfx}brc{i}", [128, 512], F32) for i in range(2)]
    yT = [S.sbuf(f"{pfx}yT{i}", [128, 512], BF16) for i in range(10)]
    hs = [S.sbuf(f"{pfx}hs{i}", [128, D], F32) for i in range(2)]
    rr = [S.sbuf(f"{pfx}r{i}", [128, D], F32) for i in range(2)]
    hn = [S.sbuf(f"{pfx}hn{i}", [128, D], F32) for i in range(2)]
    LB = LNBufs(S, pfx + "ln")
    TB = TrBufs(S, pfx + "tr", po)
    npg = 0
    for tc in range(NT // 512):
        t0 = tc * 512
        hc = hTc[tc % 2]
        S.dma("sp", f"{pfx}hTc{tc % 2}", hc.ap[:], hT_d.ap.rearrange("(k p) t -> p k t", p=128)[:, :, t0:t0 + 512],
              reads=[hT_d], writes=[hc])
        for hp in range(2):
            p = pg[npg % 2]
            npg += 1
            for kc in range(8):
                S.op("pe", "matmul", p.ap[:], lhsT=wt.ap[:, kc, 1280 + hp * 128:1280 + (hp + 1) * 128],
                                                                rhs=hc.ap[:, kc, :], start=(kc == 0), stop=(kc == 7),
                     reads=[wt, hc], writes=[p] if kc == 0 else [], accum=[p] if kc else [])
            S.op("act", "copy", out=qmT.ap[:, hp, :], in_=p.ap[:], reads=[p], accum=[qmT])
            if dbg is not None and len(dbg) > 2 and tc == 0 and hp == 0:
                dbf = S.sbuf("dbf", [128, 512], F32)
                S.op("dve", "tensor_copy", out=dbf.ap[:], in_=p.ap[:], reads=[p], writes=[dbf])
                S.dma("sp", "dbgf", dbg[1].ap[:, :], dbf.ap[:], reads=[dbf], writes=[dbg[1]])
                if len(dbg) > 2:
                    return
        for hd in range(4):
            hp, base = hd // 2, (hd % 2) * 64
            row = 64 if hd % 2 == 0 else 0
            for mc in range(2):
                ps = pS[mc]
                S.op("pe", "matmul",
                    ps.ap[:], lhsT=mkT.ap[base:base + 64, hp, mc * 128:(mc + 1) * 128],
                    rhs=qmT.ap[base:base + 64, hp, :], start=True, stop=True,
                     reads=[mkT, qmT], writes=[ps])
                S.op("act", "activation", out=pT[mc].ap[:], in_=ps.ap[:], func=AF.Exp, scale=0.125,
                     reads=[ps], writes=[pT[mc]])
            for mc in range(2):
                S.op("pe", "matmul", pO.ap[:], lhsT=mvx.ap[:, mc, hd, :], rhs=pT[mc].ap[:],
                                                           start=(mc == 0), stop=(mc == 1),
                     reads=[mvx, pT[mc]], writes=[pO] if mc == 0 else [], accum=[pO] if mc else [])
            S.op("act", "copy", out=Osb.ap[:], in_=pO.ap[:], reads=[pO], writes=[Osb])
            S.op("dve", "reciprocal", out=rd.ap[row:row + 1, :], in_=Osb.ap[row:row + 1, :],
                 reads=[Osb], writes=[rd])
            S.op("pe", "matmul", pB.ap[:], lhsT=C.ones_f.ap[row:row + 1, :], rhs=rd.ap[row:row + 1, :],
                                                  start=True, stop=True, reads=[C.ones_f, rd], writes=[pB])
            S.op("dve", "tensor_tensor", out=moT.ap[base:base + 64, hp, :],
                                                                   in0=Osb.ap[base:base + 64, :],
                                                                   in1=pB.ap[base:base + 64, :], op=ALU.mult,
                 reads=[Osb, pB], accum=[moT])
        for cc in range(10):
            p = pg[npg % 2]
            s_ = sg[npg % 2]
            npg += 1
            for kc in range(8):
                S.op("pe", "matmul", p.ap[:], lhsT=wt.ap[:, kc, cc * 128:(cc + 1) * 128],
                                                                rhs=hc.ap[:, kc, :], start=(kc == 0), stop=(kc == 7),
                     reads=[wt, hc], writes=[p] if kc == 0 else [], accum=[p] if kc else [])
            S.op("act", "activation", out=s_.ap[:], in_=p.ap[:], func=AF.Silu, reads=[p], writes=[s_])
            if cc < 8:
                b_ = brc[cc % 2]
                S.dma("pool", f"{pfx}brc{cc % 2}", b_.ap[:], brT_d.ap[cc * 128:(cc + 1) * 128, t0:t0 + 512],
                      reads=[brT_d], writes=[b_])
                S.op("dve", "tensor_tensor", out=yT[cc].ap[:], in0=b_.ap[:], in1=s_.ap[:],
                                                                          op=ALU.mult,
                     reads=[b_, s_], writes=[yT[cc]])
            else:
                S.op("dve", "tensor_tensor", out=yT[cc].ap[:], in0=moT.ap[:, cc - 8, :], in1=s_.ap[:],
                                                                   op=ALU.mult,
                     reads=[moT, s_], writes=[yT[cc]])
        if dbg is not None and tc == 0:
            for cc in range(10):
                S.dma("sp", "dbg", dbg[0].ap[cc * 128:(cc + 1) * 128, :], yT[cc].ap[:], reads=[yT[cc]], accum=[dbg[0]])
            S.dma("sp", "dbg", dbg[0].ap[1280:1408, :], hc.ap[:, 0, :], reads=[hc], accum=[dbg[0]])
            S.dma("sp", "dbg", dbg[0].ap[1408:1536, :], wt.ap[:, 0, 0:512], reads=[wt], accum=[dbg[0]])
        for sub in range(4):
            tt = tc * 4 + sub
            for half in range(2):
                for kc in range(10):
                    S.op("pe", "matmul",
                        po.ap[:, half * 512:(half + 1) * 512], lhsT=yT[kc].ap[:, sub * 128:(sub + 1) * 128],
                        rhs=wo.ap[:, kc, half * 512:(half + 1) * 512], start=(kc == 0), stop=(kc == 9),
                         reads=[yT[kc], wo], writes=[po] if (kc == 0 and half == 0) else [],
                         accum=[po] if (kc or half) else [])
            h_ = hs[tt % 2]
            S.dma("pool", f"{pfx}hs{tt % 2}", h_.ap[:], h_d.ap[tt * 128:(tt + 1) * 128, :], reads=[h_d], writes=[h_])
            r_ = rr[tt % 2]
            S.op("dve", "scalar_tensor_tensor", out=r_.ap[:], in0=h_.ap[:], scalar=ALPHA, in1=po.ap[:],
                                                                      op0=ALU.mult, op1=ALU.add,
                 reads=[h_, po], writes=[r_])
            layer_norm_tile(S, LB, tt, r_, hn[tt % 2], g_rep, b_rep)
            S.dma("sp", f"{pfx}hn{tt % 2}", hn_d.ap[tt * 128:(tt + 1) * 128, :], hn[tt % 2].ap[:], reads=[hn[tt % 2]],
                  accum=[hn_d])
            if hnT_d is not None:
                transpose_tile(S, C, TB, tt, hn[tt % 2], hnT_d)


def alloc_psum_back(S):
    return {
        "pg": [S.psum("pg0", [128, 512]), S.psum("pg1", [128, 512])],
        "pS": [S.psum("pS0", [128, 512]), S.psum("pS1", [128, 512])],
        "pO": S.psum("pO", [128, 512]),
        "pB": S.psum("pB", [128, 512]),
        "po": S.psum("po", [128, 1024]),
    }


def new_nc():
    return bass.Bass("TRN2", target_bir_lowering=False)


def build_p0():
    nc = new_nc()
    S = Sched(nc)
    ident = S.dram("ident", [128, 128], F32, kind="ExternalInput")
    x = S.dram("x", [NT, D], F32, kind="ExternalInput")
    g = S.dram("g", [D], F32, kind="ExternalInput")
    b = S.dram("b", [D], F32, kind="ExternalInput")
    h = S.dram("h", [NT, D], F32, kind="ExternalOutput")
    hT = S.dram("hT", [D, NT], BF16, kind="ExternalOutput")
    C = Ctx(S, ident)
    po = S.psum("po", [128, 1024])
    phase_ln_in(S, C, x, g.ap, b.ap, h, hT, po)
    S.finish("sp")
    S.emit()
    return nc


def build_back(last=False, debug=False):
    nc = new_nc()
    S = Sched(nc)
    ident = S.dram("ident", [128, 128], F32, kind="ExternalInput")
    hT = S.dram("hT", [D, NT], BF16, kind="ExternalInput")
    h = S.dram("h", [NT, D], F32, kind="ExternalInput")
    brT = S.dram("brT", [D, NT], F32, kind="ExternalInput")
    w_tail = S.dram("w_tail", [D, 1536], F32, kind="ExternalInput")
    w_out = S.dram("w_out", [1280, D], F32, kind="ExternalInput")
    mem = S.dram("mem", [256, D], F32, kind="ExternalInput")
    wmkv = S.dram("wmkv", [D, 512], F32, kind="ExternalInput")
    g = S.dram("g", [D], F32, kind="ExternalInput")
    b = S.dram("b", [D], F32, kind="ExternalInput")
    hn = S.dram("hn", [NT, D], F32, kind="ExternalOutput")
    hnT = None if last else S.dram("hnT", [D, NT], BF16, kind="ExternalOutput")
    C = Ctx(S, ident)
    PS = alloc_psum_back(S)
    dbg = (S.dram("dbg", [1536, 512], BF16, kind="ExternalOutput"), S.dram("dbgf", [128, 512], F32, kind="ExternalOutput")) + ((debug,) if debug >= 2 else ()) if debug else None
    phase_back(S, C, "b_", hT, h, brT, w_tail.ap, w_out.ap, mem, wmkv, g.ap, b.ap, hn, hnT, PS, dbg)
    S.finish("sp")
    S.emit()
    return nc


_CACHE = {}


def get_nc(key, fn, *a):
    if key not in _CACHE:
        _CACHE[key] = fn(*a)
    return _CACHE[key]


def run(nc, in_maps):
    res = run_bass_kernel_spmd(nc, in_maps, core_ids=list(range(NCORES)))
    return res.results


def fnet_consts(hf):
    a = np.arange(128, dtype=np.float64)
    out = {}
    w1 = np.zeros((128, 2, 64))
    tc8 = np.zeros((128, 2, 8, 32))
    ts8 = np.zeros((128, 2, 8, 32))
    for p in range(2):
        k1 = 64 * hf + 32 * p + np.arange(32, dtype=np.float64)
        ang = 2 * np.pi * np.outer(a, k1) / 128.0
        w1[:, p, :32] = np.cos(ang)
        w1[:, p, 32:] = -np.sin(ang)
        angt = 2 * np.pi * np.outer(a, k1) / 16384.0
        tc8[:, p, :, :] = np.cos(angt)[:, None, :]
        ts8[:, p, :, :] = np.sin(angt)[:, None, :]
    ang3 = 2 * np.pi * np.outer(a, a) / 128.0
    w3a = np.concatenate([np.cos(ang3), -np.sin(ang3)], 1)
    w3b = np.concatenate([np.sin(ang3), np.cos(ang3)], 1)
    c = np.arange(256, dtype=np.float64)
    angc = 2 * np.pi * np.outer(c, c) / 256.0
    out["w1"] = w1
    out["tc8"] = tc8
    out["ts8"] = ts8
    out["w3a"] = w3a
    out["w3b"] = w3b
    out["cc"] = np.cos(angc)
    out["sc"] = np.sin(angc)
    return {k: np.ascontiguousarray(v.astype(np.float32)) for k, v in out.items()}


def build_fnet():
    nc = new_nc()
    S = Sched(nc)
    hA = S.dram("hA", [128, 256, 128], F32, kind="ExternalInput")
    w1_d = S.dram("w1", [128, 2, 64], F32, kind="ExternalInput")
    tc_d = S.dram("tc8", [128, 2, 8, 32], F32, kind="ExternalInput")
    ts_d = S.dram("ts8", [128, 2, 8, 32], F32, kind="ExternalInput")
    w3a_d = S.dram("w3a", [128, 256], F32, kind="ExternalInput")
    w3b_d = S.dram("w3b", [128, 256], F32, kind="ExternalInput")
    cc_d = S.dram("cc", [256, 256], F32, kind="ExternalInput")
    sc_d = S.dram("sc", [256, 256], F32, kind="ExternalInput")
    out_d = S.dram("brO", [256, 64, 128], F32, kind="ExternalOutput")

    def ld(name, d, shape, src=None):
        t = S.sbuf(name, shape, F32)
        S.dma("sp", name, t.ap[:], d.ap if src is None else src, writes=[t])
        return t

    w1 = ld("f_w1", w1_d, [128, 2, 64])
    tc8 = ld("f_tc", tc_d, [128, 2, 8, 32])
    ts8 = ld("f_ts", ts_d, [128, 2, 8, 32])
    w3a = ld("f_w3a", w3a_d, [128, 256])
    w3b = ld("f_w3b", w3b_d, [128, 256])
    cc = ld("f_cc", cc_d, [128, 2, 256], cc_d.ap.rearrange("(j p) c -> p j c", p=128))
    sc = ld("f_sc", sc_d, [128, 2, 256], sc_d.ap.rearrange("(j p) c -> p j c", p=128))
    hin = [S.sbuf(f"f_hin{i}", [128, 32, 128], F32) for i in range(2)]
    Ur = S.sbuf("f_Ur", [128, 32, 256], F32)
    Ui = S.sbuf("f_Ui", [128, 32, 256], F32)
    Vr = [S.sbuf(f"f_Vr{j}", [128, 32, 128], F32) for j in range(2)]
    Vi = [S.sbuf(f"f_Vi{j}", [128, 32, 128], F32) for j in range(2)]
    tmp = [[S.sbuf(f"f_tmp{i}_{j}", [128, 8, 32], F32) for j in range(4)] for i in range(2)]
    osb = [S.sbuf(f"f_osb{i}", [128, 512], F32) for i in range(2)]
    pu = [S.psum(f"f_pu{i}", [128, 8, 64]) for i in range(2)]
    pv = [S.psum(f"f_pv{i}", [128, 256]) for i in range(2)]
    pc = [S.psum(f"f_pc{i}", [128, 512]) for i in range(2)]
    nin = 0
    for p in range(2):
        for cb in range(8):
            hb = hin[nin % 2]
            S.dma("sp" if nin % 2 == 0 else "pool", f"f_hin{nin % 2}", hb.ap[:], hA.ap[:, cb * 32:(cb + 1) * 32, :],
                  writes=[hb])
            nin += 1
            for q in range(4):
                ch0 = cb * 32 + q * 8
                u = pu[q % 2]
                tm = tmp[q % 2]
                for c8 in range(8):
                    S.op("pe", "matmul",
                        u.ap[:, c8, :], lhsT=hb.ap[:, q * 8 + c8, :], rhs=w1.ap[:, p, :], start=True, stop=True,
                         reads=[hb, w1], writes=[u] if c8 == 0 else [], accum=[u] if c8 else [])
                ur, ui = u.ap[:, :, 0:32], u.ap[:, :, 32:64]
                S.op("dve", "tensor_tensor", out=tm[0].ap[:], in0=ur, in1=tc8.ap[:, p], op=ALU.mult,
                     reads=[u, tc8], writes=[tm[0]])
                S.op("dve", "tensor_tensor", out=tm[1].ap[:], in0=ui, in1=ts8.ap[:, p], op=ALU.mult,
                     reads=[u, ts8], writes=[tm[1]])
                S.op("dve", "tensor_tensor", out=tm[2].ap[:], in0=ui, in1=tc8.ap[:, p], op=ALU.mult,
                     reads=[u, tc8], writes=[tm[2]])
                S.op("dve", "tensor_tensor", out=tm[3].ap[:], in0=ur, in1=ts8.ap[:, p], op=ALU.mult,
                     reads=[u, ts8], writes=[tm[3]])
                S.op("pool", "tensor_tensor",
                    out=Ur.ap[:, :, ch0:ch0 + 8].rearrange("b k c -> b c k"), in0=tm[0].ap[:], in1=tm[1].ap[:], op=ALU.add,
                     reads=[tm[0], tm[1]], accum=[Ur])
                S.op("pool", "tensor_tensor",
                    out=Ui.ap[:, :, ch0:ch0 + 8].rearrange("b k c -> b c k"), in0=tm[2].ap[:], in1=tm[3].ap[:], op=ALU.subtract,
                     reads=[tm[2], tm[3]], accum=[Ui])
        n3 = 0
        for k1 in range(32):
            for j in range(2):
                v = pv[n3 % 2]
                n3 += 1
                S.op("pe", "matmul", v.ap[:], lhsT=Ur.ap[:, k1, j * 128:(j + 1) * 128], rhs=w3a.ap[:],
                                                              start=True, stop=False, reads=[Ur, w3a], writes=[v])
                S.op("pe", "matmul", v.ap[:], lhsT=Ui.ap[:, k1, j * 128:(j + 1) * 128], rhs=w3b.ap[:],
                                                              start=False, stop=True, reads=[Ui, w3b], accum=[v])
                S.op("act", "copy", out=Vr[j].ap[:, k1, :], in_=v.ap[:, 0:128], reads=[v], accum=[Vr[j]])
                S.op("act", "copy", out=Vi[j].ap[:, k1, :], in_=v.ap[:, 128:256], reads=[v], accum=[Vi[j]])
        nq = 0
        for kq in range(8):
            for co in range(2):
                pcc = pc[nq % 2]
                ob = osb[nq % 2]
                nq += 1
                i = 0
                for j in range(2):
                    for (V, M) in ((Vr, cc), (Vi, sc)):
                        S.op("pe", "matmul",
                            pcc.ap[:], lhsT=M.ap[:, j, co * 128:(co + 1) * 128],
                            rhs=V[j].ap[:, kq * 4:(kq + 1) * 4, :].rearrange("c k t -> c (k t)"),
                            start=(i == 0), stop=(i == 3),
                             reads=[V[j], M], writes=[pcc] if i == 0 else [], accum=[pcc] if i else [])
                        i += 1
                S.op("act", "mul", out=ob.ap[:], in_=pcc.ap[:], mul=1.0 / 2048.0, reads=[pcc], writes=[ob])
                k0 = p * 32 + kq * 4
                S.dma("sp", f"f_out{nq % 2}", out_d.ap[co * 128:(co + 1) * 128, k0:k0 + 4, :].rearrange("c k t -> c (k t)"),
                      ob.ap[:], reads=[ob], accum=[out_d])
    S.finish("sp")
    S.emit()
    return nc


def fnet_layout_in(h):
    ims = []
    h3 = h.reshape(128, 128, D)
    for c in range(NCORES):
        g, hf = c // 2, c % 2
        hA = np.ascontiguousarray(h3[:, :, g * 256:(g + 1) * 256].transpose(0, 2, 1))
        d = {"hA": hA}
        d.update(fnet_consts(hf))
        ims.append(d)
    return ims


def fnet_layout_out(res):
    brT = np.empty((D, 128, 128), np.float32)
    for c in range(NCORES):
        g, hf = c // 2, c % 2
        brT[g * 256:(g + 1) * 256, :, hf * 64:(hf + 1) * 64] = res[c]["brO"].transpose(0, 2, 1)
    return brT.reshape(D, SEQ)


def rope_inv_freq(dim):
    return (np.float32(10000.0) ** (-(np.arange(0, dim, 2, dtype=np.float32) / np.float32(dim)))).astype(np.float32)


def gqa_tables(core):
    t = np.arange(core * NT, (core + 1) * NT)
    inv = rope_inv_freq(32)
    ar = ((t // 64).astype(np.float32)[:, None] * inv[None, :]).astype(np.float32).astype(np.float64)
    ac = ((t % 64).astype(np.float32)[:, None] * inv[None, :]).astype(np.float32).astype(np.float64)
    cosT = np.zeros((64, NT))
    sinT = np.zeros((64, NT))
    for d in range(64):
        a = ar if d < 32 else ac
        f = d % 16
        cosT[d] = np.cos(a[:, f])
        sinT[d] = np.sin(a[:, f]) * (-1.0 if (d % 32) < 16 else 1.0)
    cosT = np.concatenate([cosT, cosT], 0).astype(np.float32)
    sinT = np.concatenate([sinT, sinT], 0).astype(np.float32)
    blk = np.zeros((128, 128), np.float32)
    blk[:64, :64] = 1.0
    blk[64:, 64:] = 1.0
    return {"cosT": np.ascontiguousarray(cosT), "sinT": np.ascontiguousarray(sinT), "blk": blk}


def partner_swap_copy(S, eng, out3, in3, reads, accum):
    o = out3.rearrange("p (m s i) -> p m s i", s=2, i=16)
    i_ = in3.rearrange("p (m s i) -> p m s i", s=2, i=16)
    for s in range(2):
        S.op(eng, "copy" if eng == "act" else "tensor_copy", out=o[:, :, s, :], in_=i_[:, :, 1 - s, :], reads=reads, accum=accum)


def build_gqa1():
    nc = new_nc()
    S = Sched(nc)
    ident = S.dram("ident", [128, 128], F32, kind="ExternalInput")
    hT_d = S.dram("hT", [D, NT], BF16, kind="ExternalInput")
    w_d = S.dram("w_in", [D, 3072], F32, kind="ExternalInput")
    qg_d = S.dram("qg", [64], F32, kind="ExternalInput")
    kg_d = S.dram("kg", [64], F32, kind="ExternalInput")
    cos_d = S.dram("cosT", [128, NT], F32, kind="ExternalInput")
    sin_d = S.dram("sinT", [128, NT], F32, kind="ExternalInput")
    blk_d = S.dram("blk", [128, 128], F32, kind="ExternalInput")
    qT_d = S.dram("qT", [1024, NT], BF16, kind="ExternalOutput")
    kT_d = S.dram("kT", [256, NT], BF16, kind="ExternalOutput")
    v_d = S.dram("v", [NT, 4, 65], BF16, kind="ExternalOutput")
    C = Ctx(S, ident)
    wqk = S.sbuf("g_wqk", [128, 8, 1280], BF16)
    wqkp = S.sbuf("g_wqkp", [128, 8, 1280], BF16)
    wv = S.sbuf("g_wv", [128, 8, 256], BF16)
    for kc in range(8):
        st = C.wst[kc % 2]
        S.dma("sp", f"wst{kc % 2}", st.ap[:, :], w_d.ap[kc * 128:(kc + 1) * 128, 0:1536], writes=[st])
        for kp in range(2):
            src = st.ap[:, kp * 512:(kp + 1) * 512].rearrange("p (s i d) -> p i s d", s=2, i=4)
            dst = wqk.ap[:, kc, kp * 512:(kp + 1) * 512].rearrange("p (i s d) -> p i s d", s=2, i=4)
            S.op("pool", "tensor_copy", out=dst, in_=src, reads=[st], accum=[wqk])
        S.op("act", "copy", out=wqk.ap[:, kc, 1024:1280], in_=st.ap[:, 1024:1280], reads=[st], accum=[wqk])
        S.op("act", "copy", out=wv.ap[:, kc, :], in_=st.ap[:, 1280:1536], reads=[st], accum=[wv])
        partner_swap_copy(S, "dve", wqkp.ap[:, kc, :], wqk.ap[:, kc, :], reads=[wqk], accum=[wqkp])
    cosT = S.sbuf("g_cos", [128, NT], F32)
    sinT = S.sbuf("g_sin", [128, NT], F32)
    S.dma("sp", "g_cos", cosT.ap[:], cos_d.ap, writes=[cosT])
    S.dma("sp", "g_sin", sinT.ap[:], sin_d.ap, writes=[sinT])
    blk = S.sbuf("g_blk", [128, 128], F32)
    S.dma("sp", "g_blk", blk.ap[:], blk_d.ap, writes=[blk])
    gcol = S.sbuf("g_gcol", [128, 4], F32)
    for j, gd in enumerate((qg_d, kg_d)):
        g2 = gd.ap.rearrange("(d o) -> d o", o=1)
        for half in range(2):
            S.dma("sp", "g_gcol", gcol.ap[half * 64:(half + 1) * 64, 2 * j:2 * j + 1], g2, accum=[gcol])
            for b in range(4):
                pb = b ^ 1
                S.dma("sp", "g_gcol", gcol.ap[half * 64 + b * 16:half * 64 + (b + 1) * 16, 2 * j + 1:2 * j + 2],
                      g2[pb * 16:(pb + 1) * 16, :], accum=[gcol])
    CG, SG = [], []
    for j in range(2):
        cg = S.sbuf(f"g_CG{j}", [128, NT], F32)
        sg_ = S.sbuf(f"g_SG{j}", [128, NT], F32)
        S.op("dve", "tensor_scalar", out=cg.ap[:], in0=cosT.ap[:], scalar1=gcol.ap[:, 2 * j:2 * j + 1], scalar2=None,
             op0=ALU.mult, reads=[cosT, gcol], writes=[cg])
        S.op("pool", "tensor_scalar", out=sg_.ap[:], in0=sinT.ap[:], scalar1=gcol.ap[:, 2 * j + 1:2 * j + 2], scalar2=None,
             op0=ALU.mult, reads=[sinT, gcol], writes=[sg_])
        CG.append(cg)
        SG.append(sg_)
    hT = S.sbuf("g_hT", [128, 8, NT], BF16)
    for kc in range(8):
        S.dma("sp" if kc % 2 == 0 else "pool", f"g_hT{kc % 2}", hT.ap[:, kc, :], hT_d.ap[kc * 128:(kc + 1) * 128, :], accum=[hT])
    pA = [S.psum(f"g_pA{i}", [128, 512]) for i in range(2)]
    pBp = [S.psum(f"g_pBp{i}", [128, 512]) for i in range(2)]
    pss = [S.psum(f"g_pss{i}", [128, 512]) for i in range(2)]
    pv = S.psum("g_pv", [128, 256])
    sq = [S.sbuf(f"g_sq{i}", [128, 512], F32) for i in range(2)]
    rstd = [S.sbuf(f"g_rstd{i}", [128, 512], F32) for i in range(2)]
    t1 = [S.sbuf(f"g_t1{i}", [128, 512], F32) for i in range(2)]
    t2 = [S.sbuf(f"g_t2{i}", [128, 512], F32) for i in range(2)]
    ob = [S.sbuf(f"g_ob{i}", [128, NT], BF16) for i in range(2)]
    n = 0
    for cc in range(10):
        j = 0 if cc < 8 else 1
        o_ = ob[cc % 2]
        for tc in range(NT // 512):
            ts = slice(tc * 512, (tc + 1) * 512)
            a, bp, ss_ = pA[n % 2], pBp[n % 2], pss[n % 2]
            sq_, rs_, t1_, t2_ = sq[n % 2], rstd[n % 2], t1[n % 2], t2[n % 2]
            n += 1
            for kc in range(8):
                S.op("pe", "matmul", a.ap[:], lhsT=wqk.ap[:, kc, cc * 128:(cc + 1) * 128], rhs=hT.ap[:, kc, ts],
                     start=(kc == 0), stop=(kc == 7), reads=[wqk, hT], writes=[a] if kc == 0 else [], accum=[a] if kc else [])
            for kc in range(8):
                S.op("pe", "matmul", bp.ap[:], lhsT=wqkp.ap[:, kc, cc * 128:(cc + 1) * 128], rhs=hT.ap[:, kc, ts],
                     start=(kc == 0), stop=(kc == 7), reads=[wqkp, hT], writes=[bp] if kc == 0 else [], accum=[bp] if kc else [])
            S.op("act", "activation", out=sq_.ap[:], in_=a.ap[:], func=AF.Square, reads=[a], writes=[sq_])
            S.op("pe", "matmul", ss_.ap[:], lhsT=blk.ap[:], rhs=sq_.ap[:], start=True, stop=True, reads=[blk, sq_], writes=[ss_])
            S.op("act", "activation", out=rs_.ap[:], in_=ss_.ap[:], func=AF.Sqrt, bias=RMS_EPS, scale=1.0 / 64.0,
                 reads=[ss_], writes=[rs_])
            S.op("dve", "reciprocal", out=rs_.ap[:], in_=rs_.ap[:], reads=[rs_], writes=[rs_])
            S.op("dve", "tensor_tensor", out=t1_.ap[:], in0=a.ap[:], in1=CG[j].ap[:, ts], op=ALU.mult, reads=[a, CG[j]], writes=[t1_])
            S.op("dve", "tensor_tensor", out=t2_.ap[:], in0=bp.ap[:], in1=SG[j].ap[:, ts], op=ALU.mult, reads=[bp, SG[j]], writes=[t2_])
            S.op("pool", "tensor_tensor", out=t1_.ap[:], in0=t1_.ap[:], in1=t2_.ap[:], op=ALU.add, reads=[t1_, t2_], writes=[t1_])
            S.op("pool", "tensor_tensor", out=o_.ap[:, ts], in0=t1_.ap[:], in1=rs_.ap[:], op=ALU.mult, reads=[t1_, rs_],
                 writes=[o_] if tc == 0 else [], accum=[o_] if tc else [])
        if cc < 8:
            S.dma("sp", f"g_qo{cc % 2}", qT_d.ap[cc * 128:(cc + 1) * 128, :], o_.ap[:], reads=[o_], accum=[qT_d])
        else:
            S.dma("sp", f"g_qo{cc % 2}", kT_d.ap[(cc - 8) * 128:(cc - 7) * 128, :], o_.ap[:], reads=[o_], accum=[kT_d])
    vx = [S.sbuf(f"g_vx{i}", [128, 4, 65], BF16) for i in range(2)]
    for i in range(2):
        S.op("dve", "memset", vx[i].ap[:], 1.0, writes=[vx[i]])
    for tt in range(NT // 128):
        v_ = vx[tt % 2]
        for kc in range(8):
            S.op("pe", "matmul", pv.ap[:], lhsT=hT.ap[:, kc, tt * 128:(tt + 1) * 128], rhs=wv.ap[:, kc, :],
                 start=(kc == 0), stop=(kc == 7), reads=[hT, wv], writes=[pv] if kc == 0 else [], accum=[pv] if kc else [])
        S.op("act", "copy", out=v_.ap[:, :, 0:64], in_=pv.ap[:].rearrange("p (h d) -> p h d", d=64), reads=[pv], accum=[v_])
        S.dma("sp", f"g_vo{tt % 2}", v_d.ap[tt * 128:(tt + 1) * 128, :, :], v_.ap[:], reads=[v_], accum=[v_d])
    S.finish("sp")
    S.emit()
    return nc


def build_gqa2():
    nc = new_nc()
    S = Sched(nc)
    qT_d = S.dram("qT", [1024, NT], BF16, kind="ExternalInput")
    kT_d = S.dram("kTf", [256, SEQ], BF16, kind="ExternalInput")
    v_d = S.dram("vf", [SEQ, 4, 65], BF16, kind="ExternalInput")
    br_d = S.dram("brT", [D, NT], F32, kind="ExternalOutput")
    ones = S.sbuf("a_ones", [128, 128], F32)
    S.op("dve", "memset", ones.ap[:], 1.0, writes=[ones])
    kT = S.sbuf("a_kT", [128, 2, SEQ], BF16)
    V = S.sbuf("a_V", [128, 128, 260], BF16)
    qT = S.sbuf("a_qT", [128, 8, NT], BF16)
    for kp in range(2):
        for part in range(4):
            sl = slice(part * 4096, (part + 1) * 4096)
            S.dma("sp" if part % 2 == 0 else "pool", f"a_kT{part % 2}", kT.ap[:, kp, sl], kT_d.ap[kp * 128:(kp + 1) * 128, sl], accum=[kT])
    vv = v_d.ap.rearrange("(kt p) h d -> p kt (h d)", p=128)
    for part in range(8):
        S.dma("sp" if part % 2 == 0 else "pool", f"a_V{part % 2}", V.ap[:, part * 16:(part + 1) * 16, :], vv[:, part * 16:(part + 1) * 16, :], accum=[V])
    for pt in range(8):
        S.dma("sp" if pt % 2 == 0 else "pool", f"a_q{pt % 2}", qT.ap[:, pt, :], qT_d.ap[pt * 128:(pt + 1) * 128, :], accum=[qT])
    pS = [S.psum(f"a_pS{i}", [128, 512]) for i in range(3)]
    pO = [S.psum(f"a_pO{i}", [128, 512]) for i in range(2)]
    pB = S.psum("a_pB", [128, 512])
    pT = [S.sbuf(f"a_pT{i}", [128, 512], BF16) for i in range(4)]
    Osb = [S.sbuf(f"a_Osb{i}", [128, 512], F32) for i in range(2)]
    rd = [S.sbuf(f"a_rd{i}", [128, 512], F32) for i in range(2)]
    outb = [S.sbuf(f"a_out{i}", [64, 512], F32) for i in range(2)]
    nS = 0
    nO = 0
    for pt in range(8):
        kp, i = pt // 4, pt % 4
        for s in range(2):
            base = 64 * s
            kvh = 2 * kp + s
            head = 8 * kp + 4 * s + i
            for qc in range(NT // 512):
                qs = slice(qc * 512, (qc + 1) * 512)
                po_ = pO[nO % 2]
                for kt in range(128):
                    ps_ = pS[nS % 3]
                    pt_ = pT[nS % 4]
                    nS += 1
                    S.op("pe", "matmul", ps_.ap[:], lhsT=kT.ap[base:base + 64, kp, kt * 128:(kt + 1) * 128],
                         rhs=qT.ap[base:base + 64, pt, qs], start=True, stop=True, reads=[kT, qT], writes=[ps_])
                    S.op("act", "activation", out=pt_.ap[:], in_=ps_.ap[:], func=AF.Exp, scale=0.125, reads=[ps_], writes=[pt_])
                    S.op("pe", "matmul", po_.ap[0:65, :], lhsT=V.ap[:, kt, kvh * 65:(kvh + 1) * 65], rhs=pt_.ap[:],
                         start=(kt == 0), stop=(kt == 127), reads=[V, pt_], writes=[po_] if kt == 0 else [],
                         accum=[po_] if kt else [])
                o_, r_, ob_ = Osb[nO % 2], rd[nO % 2], outb[nO % 2]
                nO += 1
                S.op("act", "copy", out=o_.ap[0:65, :], in_=po_.ap[0:65, :], reads=[po_], writes=[o_])
                S.op("dve", "reciprocal", out=r_.ap[64:65, :], in_=o_.ap[64:65, :], reads=[o_], writes=[r_])
                S.op("pe", "matmul", pB.ap[0:64, :], lhsT=ones.ap[64:65, 0:64], rhs=r_.ap[64:65, :], start=True, stop=True,
                     reads=[ones, r_], writes=[pB])
                S.op("dve", "tensor_tensor", out=ob_.ap[:], in0=o_.ap[0:64, :], in1=pB.ap[0:64, :], op=ALU.mult,
                     reads=[o_, pB], writes=[ob_])
                S.dma("sp", f"a_o{nO % 2}", br_d.ap[head * 64:(head + 1) * 64, qs], ob_.ap[:], reads=[ob_], accum=[br_d])
    S.finish("sp")
    S.emit()
    return nc


DIL = (1, 4, 16)


def dil_tables(core):
    t = np.arange(core * NT, (core + 1) * NT)
    inv = rope_inv_freq(64)
    a = (t.astype(np.float32)[:, None] * inv[None, :]).astype(np.float32).astype(np.float64)
    cosT = np.zeros((64, NT))
    sinT = np.zeros((64, NT))
    for d in range(64):
        cosT[d] = np.cos(a[:, d % 32])
        sinT[d] = np.sin(a[:, d % 32]) * (-1.0 if d < 32 else 1.0)
    cosT = np.concatenate([cosT, cosT], 0).astype(np.float32)
    sinT = np.concatenate([sinT, sinT], 0).astype(np.float32)
    return {"cosT": np.ascontiguousarray(cosT), "sinT": np.ascontiguousarray(sinT)}


def build_dil1():
    nc = new_nc()
    S = Sched(nc)
    hT_d = S.dram("hT", [D, NT], BF16, kind="ExternalInput")
    w_d = S.dram("w_in", [D, 10752], F32, kind="ExternalInput")
    cos_d = S.dram("cosT", [128, NT], F32, kind="ExternalInput")
    sin_d = S.dram("sinT", [128, NT], F32, kind="ExternalInput")
    qk_d = S.dram("qkT", [6144, NT], BF16, kind="ExternalOutput")
    v_d = S.dram("v", [NT, 48, 65], BF16, kind="ExternalOutput")
    cosT = S.sbuf("d_cos", [128, NT], F32)
    sinT = S.sbuf("d_sin", [128, NT], F32)
    S.dma("sp", "d_cos", cosT.ap[:], cos_d.ap, writes=[cosT])
    S.dma("sp", "d_sin", sinT.ap[:], sin_d.ap, writes=[sinT])
    hT = S.sbuf("d_hT", [128, 8, NT], BF16)
    for kc in range(8):
        S.dma("sp" if kc % 2 == 0 else "pool", f"d_hT{kc % 2}", hT.ap[:, kc, :], hT_d.ap[kc * 128:(kc + 1) * 128, :], accum=[hT])
    wst = [S.sbuf(f"d_wst{i}", [128, 8, 512], F32) for i in range(2)]
    wc = [S.sbuf(f"d_wc{i}", [128, 8, 512], BF16) for i in range(2)]
    wcp = [S.sbuf(f"d_wcp{i}", [128, 8, 512], BF16) for i in range(2)]
    pA = [S.psum(f"d_pA{i}", [128, 512]) for i in range(2)]
    pBp = [S.psum(f"d_pBp{i}", [128, 512]) for i in range(2)]
    pv = [S.psum(f"d_pv{i}", [128, 512]) for i in range(2)]
    t1 = [S.sbuf(f"d_t1{i}", [128, 512], F32) for i in range(2)]
    t2 = [S.sbuf(f"d_t2{i}", [128, 512], F32) for i in range(2)]
    ob = [S.sbuf(f"d_ob{i}", [128, NT], BF16) for i in range(2)]
    vx = [S.sbuf(f"d_vx{i}", [128, 8, 65], BF16) for i in range(2)]
    for i in range(2):
        S.op("dve", "memset", vx[i].ap[:], 1.0, writes=[vx[i]])
    wv3 = w_d.ap.rearrange("(k p) c -> p k c", p=128)
    n = 0
    nv = 0
    for wb in range(18):
        st, w_, wp_ = wst[wb % 2], wc[wb % 2], wcp[wb % 2]
        for half in range(2):
            S.dma("sp" if half == 0 else "pool", f"d_wst{wb % 2}{half}", st.ap[:, half * 4:(half + 1) * 4, :],
                  wv3[:, half * 4:(half + 1) * 4, wb * 512:(wb + 1) * 512], accum=[st])
        for kc in range(8):
            eng = ["pool", "act"][kc % 2]
            S.op(eng, "copy" if eng == "act" else "tensor_copy", out=w_.ap[:, kc, :], in_=st.ap[:, kc, :], reads=[st], accum=[w_])
        if wb < 12:
            for kc in range(8):
                o = wp_.ap[:, kc, :].rearrange("p (m s i) -> p m s i", s=2, i=32)
                i_ = w_.ap[:, kc, :].rearrange("p (m s i) -> p m s i", s=2, i=32)
                for s in range(2):
                    eng = ["pool", "dve"][s]
                    S.op(eng, "tensor_copy", out=o[:, :, s, :], in_=i_[:, :, 1 - s, :], reads=[w_], accum=[wp_])
            for c4 in range(4):
                cc = wb * 4 + c4
                o_ = ob[cc % 2]
                for tc in range(NT // 512):
                    ts = slice(tc * 512, (tc + 1) * 512)
                    a, bp, t1_, t2_ = pA[n % 2], pBp[n % 2], t1[n % 2], t2[n % 2]
                    n += 1
                    for kc in range(8):
                        S.op("pe", "matmul", a.ap[:], lhsT=w_.ap[:, kc, c4 * 128:(c4 + 1) * 128], rhs=hT.ap[:, kc, ts],
                             start=(kc == 0), stop=(kc == 7), reads=[w_, hT], writes=[a] if kc == 0 else [], accum=[a] if kc else [])
                    for kc in range(8):
                        S.op("pe", "matmul", bp.ap[:], lhsT=wp_.ap[:, kc, c4 * 128:(c4 + 1) * 128], rhs=hT.ap[:, kc, ts],
                             start=(kc == 0), stop=(kc == 7), reads=[wp_, hT], writes=[bp] if kc == 0 else [], accum=[bp] if kc else [])
                    S.op("dve", "tensor_tensor", out=t1_.ap[:], in0=a.ap[:], in1=cosT.ap[:, ts], op=ALU.mult, reads=[a, cosT], writes=[t1_])
                    S.op("dve", "tensor_tensor", out=t2_.ap[:], in0=bp.ap[:], in1=sinT.ap[:, ts], op=ALU.mult, reads=[bp, sinT], writes=[t2_])
                    S.op("pool", "tensor_tensor", out=o_.ap[:, ts], in0=t1_.ap[:], in1=t2_.ap[:], op=ALU.add, reads=[t1_, t2_],
                         writes=[o_] if tc == 0 else [], accum=[o_] if tc else [])
                S.dma("sp", f"d_qo{cc % 2}", qk_d.ap[cc * 128:(cc + 1) * 128, :], o_.ap[:], reads=[o_], accum=[qk_d])
        else:
            vb = wb - 12
            for tt in range(NT // 128):
                p_ = pv[nv % 2]
                v_ = vx[nv % 2]
                nv += 1
                for kc in range(8):
                    S.op("pe", "matmul", p_.ap[:], lhsT=hT.ap[:, kc, tt * 128:(tt + 1) * 128], rhs=w_.ap[:, kc, :],
                         start=(kc == 0), stop=(kc == 7), reads=[hT, w_], writes=[p_] if kc == 0 else [], accum=[p_] if kc else [])
                S.op("act", "copy", out=v_.ap[:, :, 0:64], in_=p_.ap[:].rearrange("p (h d) -> p h d", d=64), reads=[p_], accum=[v_])
                S.dma("sp", f"d_vo{nv % 2}", v_d.ap[tt * 128:(tt + 1) * 128, vb * 8:(vb + 1) * 8, :], v_.ap[:], reads=[v_], accum=[v_d])
    S.finish("sp")
    S.emit()
    return nc


def dil_masks():
    jj = np.arange(128)[:, None]
    ii = np.arange(128)[None, :]
    mA = (ii <= jj).astype(np.float32)
    mB = (jj <= ii).astype(np.float32)
    bf = ml_dtypes.bfloat16
    return {"maskA": np.ascontiguousarray(np.tile(mA, (1, 4)).astype(bf)), "maskB": np.ascontiguousarray(np.tile(mB, (1, 4)).astype(bf))}


def dil_layout(qkT_full, v_full):
    bf = ml_dtypes.bfloat16
    PADT = 1024
    kpad = np.zeros((3072, SEQ + 2 * PADT), bf)
    kpad[:, PADT:PADT + SEQ] = qkT_full[3072:]
    vpad = np.zeros((SEQ + 2 * PADT, 48, 65), bf)
    vpad[PADT:PADT + SEQ] = v_full
    ims = []
    for c in range(NCORES):
        d_ = {}
        for g, dil in enumerate(DIL):
            U = NT // dil
            q = qkT_full[g * 1024:(g + 1) * 1024, c * NT:(c + 1) * NT].reshape(1024, U, dil).transpose(0, 2, 1)
            d_[f"q{g}"] = np.ascontiguousarray(q).reshape(1024, NT)
            lo = PADT + c * NT - 64 * dil
            hi = PADT + (c + 1) * NT + 64 * dil
            k = kpad[g * 1024:(g + 1) * 1024, lo:hi].reshape(1024, U + 128, dil).transpose(0, 2, 1)
            d_[f"k{g}"] = np.ascontiguousarray(k).reshape(1024, dil * (U + 128))
            v = vpad[lo:hi, g * 16:(g + 1) * 16, :].reshape(U + 128, dil, 16, 65).transpose(1, 0, 2, 3)
            d_[f"v{g}"] = np.ascontiguousarray(v)
        d_.update(dil_masks())
        ims.append(d_)
    return ims


def build_dil2():
    nc = new_nc()
    S = Sched(nc)
    q_d, k_d, v_d = [], [], []
    for g, dil in enumerate(DIL):
        U = NT // dil
        q_d.append(S.dram(f"q{g}", [1024, NT], BF16, kind="ExternalInput"))
        k_d.append(S.dram(f"k{g}", [1024, dil * (U + 128)], BF16, kind="ExternalInput"))
        v_d.append(S.dram(f"v{g}", [dil, U + 128, 16, 65], BF16, kind="ExternalInput"))
    mA_d = S.dram("maskA", [128, 512], BF16, kind="ExternalInput")
    mB_d = S.dram("maskB", [128, 512], BF16, kind="ExternalInput")
    br_d = S.dram("brT", [D, NT], F32, kind="ExternalOutput")
    ones = S.sbuf("x_ones", [128, 128], F32)
    S.op("dve", "memset", ones.ap[:], 1.0, writes=[ones])
    mA = S.sbuf("x_mA", [128, 512], BF16)
    mB = S.sbuf("x_mB", [128, 512], BF16)
    S.dma("sp", "x_mA", mA.ap[:], mA_d.ap, writes=[mA])
    S.dma("sp", "x_mB", mB.ap[:], mB_d.ap, writes=[mB])
    acc = S.sbuf("x_acc", [128, 8, NT], F32)
    Vb = [S.sbuf(f"x_V{i}", [128, 32 * 520], BF16) for i in range(2)]
    qb = [S.sbuf(f"x_q{i}", [64, NT], BF16) for i in range(2)]
    kb = [S.sbuf(f"x_k{i}", [64, 2 * NT], BF16) for i in range(2)]
    pSA = [S.psum(f"x_pSA{i}", [128, 512]) for i in range(2)]
    pSB = [S.psum(f"x_pSB{i}", [128, 512]) for i in range(2)]
    pO = [S.psum(f"x_pO{i}", [128, 512]) for i in range(2)]
    pBc = S.psum("x_pB", [128, 512])
    PA = [S.sbuf(f"x_PA{i}", [128, 512], BF16) for i in range(2)]
    PB = [S.sbuf(f"x_PB{i}", [128, 512], BF16) for i in range(2)]
    rd = S.sbuf("x_rd", [128, 512], F32)
    outb = [S.sbuf(f"x_out{i}", [64, 512], F32) for i in range(2)]
    nV = 0
    nq = 0
    nb = 0
    nout = 0
    for hh in range(2):
        for g, dil in enumerate(DIL):
            U = NT // dil
            nj = U // 128
            ntile = nj + 1
            Vt = Vb[nV % 2]
            nV += 1
            Vv = Vt.ap[:, 0:dil * ntile * 520].rearrange("p (r m e) -> p r m e", r=dil, m=ntile)
            vsrc = v_d[g].ap[:, :, hh * 8:(hh + 1) * 8, :].rearrange("r (m p) h e -> p r m (h e)", p=128)
            for r in range(dil):
                S.dma("sp" if r % 2 == 0 else "pool", f"x_V{nV % 2}{r % 2}", Vv[:, r, :, :], vsrc[:, r, :, :], accum=[Vt])
            for hl in range(8):
                h = hh * 8 + hl
                q_, k_ = qb[nq % 2], kb[nq % 2]
                nq += 1
                kw = dil * (U + 128)
                S.dma("sp", f"x_q{nq % 2}", q_.ap[:, :], q_d[g].ap[h * 64:(h + 1) * 64, :], writes=[q_])
                S.dma("pool", f"x_k{nq % 2}", k_.ap[:, 0:kw], k_d[g].ap[h * 64:(h + 1) * 64, :], writes=[k_])
                qv = q_.ap[:, :].rearrange("p (r u) -> p r u", r=dil)
                kv = k_.ap[:, 0:kw].rearrange("p (r x) -> p r x", r=dil)
                blocks = [(r, j) for r in range(dil) for j in range(nj)]
                for b0 in range(0, 16, 4):
                    sa, sb, po_ = pSA[nb % 2], pSB[nb % 2], pO[nb % 2]
                    pa, pb = PA[nb % 2], PB[nb % 2]
                    nb += 1
                    for bi in range(4):
                        r, j = blocks[b0 + bi]
                        cs = slice(bi * 128, (bi + 1) * 128)
                        S.op("pe", "matmul", sa.ap[:, cs], lhsT=kv[:, r, 128 * j:128 * j + 128], rhs=qv[:, r, 128 * j:128 * j + 128],
                             start=True, stop=True, reads=[k_, q_], writes=[sa] if bi == 0 else [], accum=[sa] if bi else [])
                        S.op("pe", "matmul", sb.ap[:, cs], lhsT=kv[:, r, 128 * j + 128:128 * j + 256], rhs=qv[:, r, 128 * j:128 * j + 128],
                             start=True, stop=True, reads=[k_, q_], writes=[sb] if bi == 0 else [], accum=[sb] if bi else [])
                    S.op("act", "activation", out=pa.ap[:], in_=sa.ap[:], func=AF.Exp, scale=0.125, reads=[sa], writes=[pa])
                    S.op("act", "activation", out=pb.ap[:], in_=sb.ap[:], func=AF.Exp, scale=0.125, reads=[sb], writes=[pb])
                    S.op("dve", "tensor_tensor", out=pa.ap[:], in0=pa.ap[:], in1=mA.ap[:], op=ALU.mult, reads=[pa, mA], writes=[pa])
                    S.op("pool", "tensor_tensor", out=pb.ap[:], in0=pb.ap[:], in1=mB.ap[:], op=ALU.mult, reads=[pb, mB], writes=[pb])
                    for bi in range(4):
                        r, j = blocks[b0 + bi]
                        cs = slice(bi * 128, (bi + 1) * 128)
                        S.op("pe", "matmul", po_.ap[0:65, cs], lhsT=Vv[:, r, j, hl * 65:(hl + 1) * 65], rhs=pa.ap[:, cs],
                             start=True, stop=False, reads=[Vt, pa], writes=[po_] if bi == 0 else [], accum=[po_] if bi else [])
                        S.op("pe", "matmul", po_.ap[0:65, cs], lhsT=Vv[:, r, j + 1, hl * 65:(hl + 1) * 65], rhs=pb.ap[:, cs],
                             start=False, stop=True, reads=[Vt, pb], accum=[po_])
                    av = acc.ap[0:65, hl, :].rearrange("p (u r) -> p r u", r=dil)
                    r0, j0 = blocks[b0]
                    if dil == 16:
                        dst = av[:, r0:r0 + 4, :]
                        src = po_.ap[0:65, :].rearrange("p (r u) -> p r u", r=4)
                    else:
                        dst = av[:, r0, 128 * j0:128 * j0 + 512]
                        src = po_.ap[0:65, :]
                    if g == 0:
                        S.op("act", "copy", out=dst, in_=src, reads=[po_], accum=[acc])
                    else:
                        S.op("dve", "tensor_tensor", out=dst, in0=dst, in1=src, op=ALU.add, reads=[po_, acc], accum=[acc])
        for hl in range(8):
            h = hh * 8 + hl
            for qc in range(NT // 512):
                qs = slice(qc * 512, (qc + 1) * 512)
                ob_ = outb[nout % 2]
                nout += 1
                S.op("dve", "reciprocal", out=rd.ap[64:65, :], in_=acc.ap[64:65, hl, qs], reads=[acc], writes=[rd])
                S.op("pe", "matmul", pBc.ap[0:64, :], lhsT=ones.ap[64:65, 0:64], rhs=rd.ap[64:65, :], start=True, stop=True,
                     reads=[ones, rd], writes=[pBc])
                S.op("dve", "tensor_tensor", out=ob_.ap[:], in0=acc.ap[0:64, hl, qs], in1=pBc.ap[0:64, :], op=ALU.mult,
                     reads=[acc, pBc], writes=[ob_])
                S.dma("sp", f"x_o{nout % 2}", br_d.ap[h * 64:(h + 1) * 64, qs], ob_.ap[:], reads=[ob_], accum=[br_d])
    S.finish("sp")
    S.emit()
    return nc


def kernel(x, mem, ln_in_g, ln_in_b, w_mem_kv, w_in_a, w_in_b, q_norm_g, k_norm_g, w_in_c, w_out, ln_g, ln_b):
    f32 = lambda a: np.ascontiguousarray(np.asarray(a, dtype=np.float32))
    x = f32(x)[0]
    mem = f32(mem)[0]
    ln_in_g, ln_in_b, w_mem_kv = f32(ln_in_g), f32(ln_in_b), f32(w_mem_kv)
    w_in_a, w_in_b, w_in_c, w_out = f32(w_in_a), f32(w_in_b), f32(w_in_c), f32(w_out)
    q_norm_g, k_norm_g, ln_g, ln_b = f32(q_norm_g), f32(k_norm_g), f32(ln_g), f32(ln_b)
    ident = np.eye(128, dtype=np.float32)
    cs = [slice(c * NT, (c + 1) * NT) for c in range(NCORES)]

    res = run(get_nc("p0", build_p0), [{"ident": ident, "x": np.ascontiguousarray(x[cs[c]]), "g": ln_in_g, "b": ln_in_b}
                                       for c in range(NCORES)])
    h = [r["h"] for r in res]
    hT = [r["hT"] for r in res]
    for i in range(4):
        kind, j = i % 3, i // 3
        if kind == 0:
            r = run(get_nc("fnet", build_fnet), fnet_layout_in(np.concatenate(h, 0)))
            brT_full = fnet_layout_out(r)
            brT = [np.ascontiguousarray(brT_full[:, cs[c]]) for c in range(NCORES)]
            w_tail = w_in_a[j]
        elif kind == 1:
            ims = []
            for c in range(NCORES):
                d = {"ident": ident, "hT": hT[c], "w_in": w_in_b[j], "qg": q_norm_g[j], "kg": k_norm_g[j]}
                d.update(gqa_tables(c))
                ims.append(d)
            r = run(get_nc("gqa1", build_gqa1), ims)
            kT_full = np.ascontiguousarray(np.concatenate([q["kT"] for q in r], 1))
            v_full = np.ascontiguousarray(np.concatenate([q["v"] for q in r], 0))
            r2 = run(get_nc("gqa2", build_gqa2), [{"qT": r[c]["qT"], "kTf": kT_full, "vf": v_full} for c in range(NCORES)])
            brT = [q["brT"] for q in r2]
            w_tail = np.ascontiguousarray(w_in_b[j][:, 1536:])
        else:
            ims = []
            for c in range(NCORES):
                d = {"hT": hT[c], "w_in": w_in_c[j]}
                d.update(dil_tables(c))
                ims.append(d)
            r = run(get_nc("dil1", build_dil1), ims)
            qkT_full = np.concatenate([q["qkT"] for q in r], 1)
            v_full = np.concatenate([q["v"] for q in r], 0)
            r2 = run(get_nc("dil2", build_dil2), dil_layout(qkT_full, v_full))
            brT = [q["brT"] for q in r2]
            w_tail = np.ascontiguousarray(w_in_c[j][:, 9216:])
        ims = [{"ident": ident, "hT": hT[c], "h": h[c], "brT": brT[c], "w_tail": w_tail, "w_out": w_out[i], "mem": mem,
                "wmkv": w_mem_kv, "g": ln_g[i], "b": ln_b[i]} for c in range(NCORES)]
        r = run(get_nc("back", build_back), ims)
        h = [q["hn"] for q in r]
        hT = [q["hnT"] for q in r]
    return np.concatenate(h, 0)[None].astype(np.float32)
```
